# Optimizing a Trainium2 kernel written in Bass

```python
import jax, jax.numpy as jnp
from jax import lax
import numpy as np

D_MODEL = 1024
BATCH = 16
SEQ = 2048
DEPTH = 4

D_MIX = D_MODEL
D_FOURIER = D_MIX // 2
D_POOL = D_MIX - D_FOURIER
N_FOURIER_HEADS = 4
FOURIER_HEAD_DIM = D_FOURIER // N_FOURIER_HEADS
POOL_WINDOWS = (2, 4, 8, 16)
N_POOL_GROUPS = len(POOL_WINDOWS)
POOL_GROUP_DIM = D_POOL // N_POOL_GROUPS
D_FF = 128 * ((8 * D_MODEL // 3 + 127) // 128)
EPS = 1e-6

kernel_name = "hybrid_fourier_pool_macaron_encoder"


def rms_norm(x, g):
    xf = x.astype(jnp.float32)
    y = xf * lax.rsqrt(jnp.mean(xf * xf, axis=-1, keepdims=True) + EPS)
    return (y * g.astype(jnp.float32)).astype(x.dtype)


def swiglu(h, w_gate, w_up, w_down):
    a = jnp.einsum('bsd,df->bsf', h, w_gate)
    b = jnp.einsum('bsd,df->bsf', h, w_up)
    return jnp.einsum('bsf,fd->bsd', jax.nn.silu(a) * b, w_down)


def fourier_heads(u, w):
    B, S, _ = u.shape
    uh = u.reshape(B, S, N_FOURIER_HEADS, FOURIER_HEAD_DIM).astype(jnp.float32)
    f = jnp.fft.fft2(uh, axes=(1, 3), norm='ortho').real.astype(u.dtype)
    return jnp.einsum('bshc,hcd->bshd', f, w).reshape(B, S, D_FOURIER)


def centred_window_mean(u, radius):
    S = u.shape[1]
    cs = jnp.pad(jnp.cumsum(u, axis=1), ((0, 0), (1, 0), (0, 0)))
    cs = jnp.pad(cs, ((0, 0), (radius, radius), (0, 0)), mode='edge')
    win = cs[:, 2 * radius + 1:2 * radius + 1 + S] - cs[:, :S]
    t = jnp.arange(S)
    count = (jnp.minimum(t + radius, S - 1) - jnp.maximum(t - radius, 0) + 1).astype(jnp.float32)
    return win / count[None, :, None]


def pool_groups(u, w, scale):
    B, S, _ = u.shape
    uf = u.reshape(B, S, N_POOL_GROUPS, POOL_GROUP_DIM).astype(jnp.float32)
    diffs = jnp.stack(
        [centred_window_mean(uf[:, :, g], win // 2) - uf[:, :, g] for g, win in enumerate(POOL_WINDOWS)],
        axis=2).astype(u.dtype)
    y = jnp.einsum('bsgc,gcd->bsgd', diffs, w).reshape(B, S, D_POOL)
    return y * scale


def setup_inputs(seed: int = 0) -> dict:
    key = jax.random.key(seed)
    ks = jax.random.split(key, 24)

    def normal(k, shape, fan_in):
        return jax.random.normal(k, shape, jnp.float32) * (fan_in ** -0.5)

    def gain(k, shape):
        return 1.0 + 0.02 * jax.random.normal(k, shape, jnp.float32)

    L, D, F = DEPTH, D_MODEL, D_FF
    return {
        "x": jax.random.normal(ks[0], (BATCH, SEQ, D), jnp.float32),
        "ffn1_pre_g": gain(ks[1], (L, D)),
        "ffn1_w_gate": normal(ks[2], (L, D, F), D),
        "ffn1_w_up": normal(ks[3], (L, D, F), D),
        "ffn1_w_down": normal(ks[4], (L, F, D), F),
        "ffn1_post_g": gain(ks[5], (L, D)),
        "mix_pre_g": gain(ks[6], (L, D)),
        "w_in": normal(ks[7], (L, D, D_MIX), D),
        "fourier_w": normal(ks[8], (L, N_FOURIER_HEADS, FOURIER_HEAD_DIM, FOURIER_HEAD_DIM), FOURIER_HEAD_DIM),
        "pool_w": normal(ks[9], (L, N_POOL_GROUPS, POOL_GROUP_DIM, POOL_GROUP_DIM), POOL_GROUP_DIM),
        "pool_scale": gain(ks[10], (L, D_POOL)),
        "w_out": normal(ks[11], (L, D_MIX, D), D_MIX),
        "mix_post_g": gain(ks[12], (L, D)),
        "ffn2_pre_g": gain(ks[13], (L, D)),
        "ffn2_w_gate": normal(ks[14], (L, D, F), D),
        "ffn2_w_up": normal(ks[15], (L, D, F), D),
        "ffn2_w_down": normal(ks[16], (L, F, D), F),
        "ffn2_post_g": gain(ks[17], (L, D)),
    }


def reference(x, ffn1_pre_g, ffn1_w_gate, ffn1_w_up, ffn1_w_down, ffn1_post_g,
              mix_pre_g, w_in, fourier_w, pool_w, pool_scale, w_out, mix_post_g,
              ffn2_pre_g, ffn2_w_gate, ffn2_w_up, ffn2_w_down, ffn2_post_g):
    for l in range(DEPTH):
        h = rms_norm(x, ffn1_pre_g[l])
        x = x + 0.5 * rms_norm(swiglu(h, ffn1_w_gate[l], ffn1_w_up[l], ffn1_w_down[l]), ffn1_post_g[l])

        h = rms_norm(x, mix_pre_g[l])
        u = jnp.einsum('bsd,dm->bsm', h, w_in[l])
        y_fourier = fourier_heads(u[..., :D_FOURIER], fourier_w[l])
        y_pool = pool_groups(u[..., D_FOURIER:], pool_w[l], pool_scale[l])
        y = jnp.einsum('bsm,md->bsd', jnp.concatenate([y_fourier, y_pool], axis=-1), w_out[l])
        x = x + rms_norm(y, mix_post_g[l])

        h = rms_norm(x, ffn2_pre_g[l])
        x = x + 0.5 * rms_norm(swiglu(h, ffn2_w_gate[l], ffn2_w_up[l], ffn2_w_down[l]), ffn2_post_g[l])
    return x
```

```python
import contextlib
import numpy as np
import ml_dtypes
import concourse.bass as bass
import concourse.mybir as mybir
from concourse.bass_utils import run_bass_kernel_spmd

F32 = mybir.dt.float32
BF16 = mybir.dt.bfloat16
AF = mybir.ActivationFunctionType
ALU = mybir.AluOpType

D = 1024
FF = 2816
S = 2048
NCORES = 8
NSEQ = 2
KD = D // 128
NJ = FF // 128
NJP = NJ // 2
TT = 512
NSUB = 2
NSQ = 2
NSG = 1
NT = S // TT
EPS = 1e-6
PADP = 8
UPW = S + 2 * PADP
RADII = (1, 2, 4, 8)

G_F1PRE, G_F1POST, G_MPRE, G_MPOST, G_F2PRE, G_F2POST = 0, 8, 16, 24, 32, 40
G_PSCALE = 48
G_PER_LAYER = 52

GRAN = 256
PSUM_BASE = 1 << 20
SBUF_LIMIT = 212800


class Op:
    __slots__ = ("eng", "fn", "waits", "signal", "semval", "dsem", "dval", "is_dma")

    def __init__(self, eng, fn, is_dma):
        self.eng = eng
        self.fn = fn
        self.waits = []
        self.signal = False
        self.semval = 0
        self.dsem = None
        self.dval = 0
        self.is_dma = is_dma


class Prog:
    def __init__(self):
        self.ops = []
        self.lastw = {}
        self.readers = {}

    @staticmethod
    def _grans(iv):
        lo, hi = iv
        return range(lo // GRAN, (hi - 1) // GRAN + 1)

    def add(self, eng, fn, reads=(), writes=(), dsem=None, dval=0):
        idx = len(self.ops)
        is_dma = dsem is not None
        op = Op(eng, fn, is_dma)
        op.dsem, op.dval = dsem, dval
        raw = set()
        other = set()
        for iv in reads:
            for g in self._grans(iv):
                w = self.lastw.get(g)
                if w is not None:
                    raw.add(w)
        for iv in writes:
            for g in self._grans(iv):
                w = self.lastw.get(g)
                if w is not None:
                    other.add(w)
                rd = self.readers.get(g)
                if rd:
                    other.update(rd.values())
        other -= raw
        ops = self.ops
        for y in raw:
            oy = ops[y]
            if (not is_dma) and (not oy.is_dma) and oy.eng == eng and eng == "pe":
                continue
            op.waits.append(y)
        for y in other:
            oy = ops[y]
            if (not is_dma) and (not oy.is_dma) and oy.eng == eng and eng == "pe":
                continue
            op.waits.append(y)
        for y in op.waits:
            if not ops[y].is_dma:
                ops[y].signal = True
        key = ("d", idx) if is_dma else eng
        for iv in reads:
            for g in self._grans(iv):
                self.readers.setdefault(g, {})[key] = idx
        for iv in writes:
            for g in self._grans(iv):
                self.lastw[g] = idx
                self.readers[g] = {}
        self.ops.append(op)
        return idx

    def finalize(self):
        cnt = {}
        for op in self.ops:
            if op.is_dma:
                continue
            if op.signal:
                cnt[op.eng] = cnt.get(op.eng, 0) + 1
                op.semval = cnt[op.eng]

    def emit(self, eng_name, eng, engsems, extra_waits=()):
        waited = {}
        ops = self.ops
        for op in ops:
            if op.eng != eng_name:
                continue
            for y in op.waits:
                oy = ops[y]
                if oy.is_dma:
                    sem, val = oy.dsem, oy.dval
                else:
                    sem, val = engsems[oy.eng], oy.semval
                k = id(sem)
                if waited.get(k, 0) < val:
                    eng.wait_ge(sem, val)
                    waited[k] = val
            ins = op.fn(eng)
            if op.is_dma:
                ins.then_inc(op.dsem, 16)
            elif op.signal:
                ins.then_inc(engsems[eng_name], 1)
        for sem, val in extra_waits:
            eng.wait_ge(sem, val)


class Buf:
    def __init__(self, arena, off, nbytes, dtype):
        assert off % 4 == 0 and nbytes % 4 == 0
        self.off = off
        self.nbytes = nbytes
        self.dtype = dtype
        self.esz = 2 if dtype == BF16 else 4
        v = arena[:, off // 4:(off + nbytes) // 4]
        self.ap = v.bitcast(BF16) if dtype == BF16 else v
        self.n = nbytes // self.esz

    def iv(self, e0=0, e1=None):
        if e1 is None:
            e1 = self.n
        return (self.off + e0 * self.esz, self.off + e1 * self.esz)

    def sl(self, e0, e1):
        return self.ap[:, e0:e1]


def build_program(L):
    nc = bass.Bass("TRN2", target_bir_lowering=False)
    NG = G_PER_LAYER * L

    def din(name, shape, dt):
        return nc.dram_tensor(name, list(shape), dt, kind="ExternalInput").ap()

    xT = din("xT", [NSEQ, KD, 128, S], F32)
    wg = [din("wg1", [L, D, FF], F32), din("wg2", [L, D, FF], F32)]
    wu = [din("wu1", [L, D, FF], F32), din("wu2", [L, D, FF], F32)]
    wd = [din("wd1", [L, FF, D], F32), din("wd2", [L, FF, D], F32)]
    w_in = din("w_in", [L, D, D], F32)
    w_out = din("w_out", [L, D, D], F32)
    fw_d = din("fw", [L, 4, 128, 128], F32)
    pw_d = din("pw", [L, 4, 128, 128], F32)
    gtab_d = din("gtab", [128, NG], F32)
    ccsc_d = din("ccsc", [128, 256], BF16)
    tw_d = din("tw", [2, 8, 128, 2, 1024], BF16)
    inv_d = din("invtab", [128, 32], F32)
    outT = nc.dram_tensor("outT", [NSEQ, KD, 128, S], F32, kind="ExternalOutput").ap()

    P = Prog()
    es = contextlib.ExitStack()
    with es:
        arena = es.enter_context(nc.sbuf_tensor("arena", [128, SBUF_LIMIT // 4], F32))
        banks = [es.enter_context(nc.psum_tensor("bank%d" % b, [128, 512], F32)) for b in range(8)]

        def bank_iv(b):
            return (PSUM_BASE + b * 2048, PSUM_BASE + (b + 1) * 2048)

        nsem = [0]

        def newsem(name):
            nsem[0] += 1
            return es.enter_context(nc.semaphore(name))

        engsems = {e: newsem("sem_" + e) for e in ("pe", "act", "dve", "pool", "sp")}

        cur = [0]

        def alloc(nbytes, dtype):
            nb = (nbytes + GRAN - 1) // GRAN * GRAN
            b = Buf(arena, cur[0], nbytes, dtype)
            cur[0] += nb
            return b

        def at(off, nbytes, dtype):
            return Buf(arena, off, nbytes, dtype)

        xres = alloc(KD * S * 4, F32)
        SH = cur[0]
        hbuf = [at(SH + i * 8192, 8192, BF16) for i in range(2)]
        ACT_SZ = NJ * NSUB * TT * 2
        actb = at(SH + 16384, ACT_SZ, BF16)
        ysb_f = [at(SH + 16384 + ACT_SZ + i * KD * TT * 4, KD * TT * 4, F32) for i in range(NSUB)]
        FFN_END = SH + 16384 + ACT_SZ + NSUB * KD * TT * 4
        uF = at(SH + 16384, 4 * S * 2, BF16)
        UP0 = SH + 32768
        uP = at(UP0, 4 * UPW * 4, F32)
        UPSZ = (4 * UPW * 4 + GRAN - 1) // GRAN * GRAN
        Abuf = at(UP0, 4 * 16 * 256 * 2, BF16)
        ysb_m = [at(UP0 + i * KD * TT * 4, KD * TT * 4, F32) for i in range(2)]
        diffs = at(UP0 + UPSZ, S * 2, BF16)
        ymixF = at(UP0 + UPSZ, 4 * TT * 2, BF16)
        ymixP = at(UP0 + UPSZ + 4096, 4 * S * 2, BF16)
        MIX_END = UP0 + UPSZ + 4096 + 16384
        fwb = at(MIX_END, 4 * 128 * 2, BF16)
        pwb = at(MIX_END + 1024, 4 * 128 * 2, BF16)
        diffs2 = at(MIX_END + 2048, S * 2, BF16)
        assert MIX_END + 2048 + S * 2 <= FFN_END
        SH_END = max(FFN_END, MIX_END + 2048)
        twring_off = SH
        cur[0] = SH_END
        GU_SLOT = 8192
        DN_SLOT = NJ * 256 * 2
        NGU, NDN, NTW = 2, 2, 4
        gu_slots = [alloc(GU_SLOT, BF16) for _ in range(NGU)]
        dn_slots = [alloc(DN_SLOT, BF16) for _ in range(NDN)]
        tmpA = at(dn_slots[0].off, UPW * 4, F32)
        tmpB = at(dn_slots[0].off + UPW * 4, UPW * 4, F32)
        assert tmpB.off + tmpB.nbytes <= dn_slots[-1].off + DN_SLOT
        tw_slots = [at(twring_off + i * 4096, 4096, BF16) for i in range(NTW)]
        sgb = [alloc(TT * 4, F32) for _ in range(NSG)]
        sqb = [alloc(TT * 2, BF16) for _ in range(NSQ)]
        accb = [alloc(TT * 2, BF16) for _ in range(3)]
        rstdb = [alloc(TT * 4, F32) for _ in range(2)]
        edge = at(sgb[0].off, 64, F32)
        c0_ = cur[0]
        gtab = at(c0_, NG * 4, F32)
        c0_ += (NG * 4 + 31) // 32 * 32
        ones_a = at(c0_, 256, BF16)
        ones_b = at(c0_ + 256, 256, BF16)
        ccsc = at(c0_ + 512, 512, BF16)
        invt = at(c0_ + 1024, 128, F32)
        cur[0] = c0_ + 1152
        assert cur[0] <= SBUF_LIMIT, cur[0]

        def x_ap(k, t0, t1):
            return xres.ap[:, k * S + t0:k * S + t1]

        def x_iv(k, t0, t1):
            return xres.iv(k * S + t0, k * S + t1)

        dma_cnt = {}

        def dma(queue, out_ap, in_ap, sem, reads=(), writes=(), group_final=None):
            k = id(sem)
            if group_final is None:
                dma_cnt[k] = dma_cnt.get(k, 0) + 16
                val = dma_cnt[k]
            else:
                val = group_final
            if queue == "pool":
                fn = lambda e, o=out_ap, i=in_ap: e.dma_start(out=o, in_=i)
            else:
                fn = lambda e, o=out_ap, i=in_ap: e.dma_start(out=o, in_=i)
            return P.add(queue, fn, reads=reads, writes=writes, dsem=sem, dval=val)

        def dma_group(queue, sem, parts):
            k = id(sem)
            final = dma_cnt.get(k, 0) + 16 * len(parts)
            dma_cnt[k] = final
            for (o, i, r, w) in parts:
                dma(queue, o, i, sem, reads=r, writes=w, group_final=final)

        class Ring:
            def __init__(self, name, slots, queue):
                self.slots = slots
                self.queue = queue
                self.sems = [newsem("%s_%d" % (name, i)) for i in range(len(slots))]
                self.free = list(range(len(slots)))
                self.pending = []
                self.ready = []
                self.hold = False

            def push(self, pieces):
                self.pending.extend(pieces)
                self.pump()

            def open_barrier(self):
                assert self.pending and self.pending[0] is None
                self.pending.pop(0)
                self.pump()

            def pump(self):
                while self.free and self.pending:
                    if self.pending[0] is None:
                        break
                    s = self.free.pop(0)
                    mk = self.pending.pop(0)
                    dma_group(self.queue, self.sems[s], mk(self.slots[s]))
                    self.ready.append(s)

            def acquire(self):
                assert self.ready, "ring underflow"
                return self.ready.pop(0)

            def release(self, s):
                self.free.append(s)
                self.pump()

        gu = Ring("gu", gu_slots, "pool")
        dn = Ring("dn", dn_slots, "pool")
        tw = Ring("tw", tw_slots, "sp")

        def mk_gateup(f, l, jp):
            def mk(slot):
                parts = []
                for wi, w in enumerate((wg[f], wu[f])):
                    src = w[l, :, jp * 256:(jp + 1) * 256].rearrange("(k p) m -> p k m", p=128)
                    dst = slot.ap[:, wi * 2048:(wi + 1) * 2048].rearrange("p (k m) -> p k m", k=KD)
                    parts.append((dst, src, (), (slot.iv(wi * 2048, (wi + 1) * 2048),)))
                return parts
            return mk

        def mk_sq(wdram, l, cp):
            def mk(slot):
                src = wdram[l, :, cp * 512:(cp + 1) * 512].rearrange("(k p) m -> p k m", p=128)
                dst = slot.ap[:, 0:4096].rearrange("p (k m) -> p k m", k=KD)
                return [(dst, src, (), (slot.iv(0, 4096),))]
            return mk

        def mk_down(f, l, ip):
            def mk(slot):
                parts = []
                src = wd[f][l, :, ip * 256:(ip + 1) * 256].rearrange("(j p) m -> p j m", p=128)
                dst = slot.ap.rearrange("p (j m) -> p j m", j=NJ)
                for (j0, j1) in ((0, 11), (11, 22)):
                    parts.append((dst[:, j0:j1, :], src[:, j0:j1, :], (), (slot.iv(j0 * 256, j1 * 256),)))
                return parts
            return mk

        def mk_tw(kt, scp):
            def mk(slot):
                src = tw_d[kt, scp]
                dst = slot.ap.rearrange("p (a m) -> p a m", a=2)
                return [(dst, src, (), (slot.iv(),))]
            return mk

        rr = {"sq": 0, "sg": 0, "rstd": 0, "ss": 0, "gu_bank": 0, "y_bank": 0, "ev": 0}
        SS_BANKS = (6, 7)

        def mm(bank_ap, bank_ivs, lhsT, l_iv, rhs, r_iv, start, stop):
            P.add("pe", lambda e, o=bank_ap, a=lhsT, b=rhs, s0=start, s1=stop:
                  e.matmul(o, a, b, start=s0, stop=s1),
                  reads=(l_iv, r_iv), writes=(bank_ivs,))

        def act_op(out_ap, out_iv, in_ap, in_iv, func, scale=None, extra_reads=()):
            if scale is None:
                fn = lambda e, o=out_ap, i=in_ap, f=func: e.activation(out=o, in_=i, func=f)
            else:
                fn = lambda e, o=out_ap, i=in_ap, f=func, s=scale: e.activation(out=o, in_=i, func=f, scale=s)
            P.add("act", fn, reads=(in_iv,) + tuple(extra_reads), writes=(out_iv,))

        def gcol(c):
            return gtab.ap[:, c:c + 1], gtab.iv(c, c + 1)

        sq_pending = []
        deferred = []
        rstd_busy = [0, 0]

        def pump_deferred(n):
            while n > 0 and deferred:
                deferred.pop(0)[1]()
                n -= 1

        def flush_deferred(keep=0):
            while len(deferred) > keep:
                deferred.pop(0)[1]()

        def flush_deferred_tiles(tiles):
            last = -1
            for n_, (t_, _) in enumerate(deferred):
                if t_ in tiles:
                    last = n_
            for _ in range(last + 1):
                deferred.pop(0)[1]()

        def rstd_alloc():
            i = rr["rstd"] % 2
            rr["rstd"] += 1
            if rstd_busy[i]:
                flush_deferred()
            assert rstd_busy[i] == 0
            return i, rstdb[i]

        def sq_alloc():
            q = sqb[rr["sq"] % NSQ]
            rr["sq"] += 1
            return q

        def sq_accumulate(acc, first, in_ap, in_iv):
            if first:
                act_op(acc.ap, acc.iv(), in_ap, in_iv, AF.Square)
            else:
                q = sq_alloc()
                act_op(q.ap, q.iv(), in_ap, in_iv, AF.Square)
                P.add("dve", lambda e, o=acc.ap, b=q.ap: e.tensor_tensor(o, o, b, ALU.add),
                      reads=(acc.iv(), q.iv()), writes=(acc.iv(),))

        def flush_sq():
            while sq_pending:
                sq_pending.pop(0)()

        def ss_mm(bank, ones, q, start, stop, after=None):
            def reg():
                mm(banks[bank][:, :], bank_iv(bank), ones.ap, ones.iv(), q.ap, q.iv(), start, stop)
                if after is not None:
                    after()
            sq_pending.append(reg)

        class PJob:
            def __init__(self, tile, gbase, hb, bank):
                self.tile, self.gbase, self.hb, self.bank = tile, gbase, hb, bank
                self.acc = accb[2]
                self.nsq = 0
                self.done = False

            def squares_left(self):
                return KD - self.nsq

            def square(self):
                t0, t1 = self.tile * TT, (self.tile + 1) * TT
                k = self.nsq
                self.nsq += 1
                if k == 0:
                    flush_sq()
                sq_accumulate(self.acc, k == 0, x_ap(k, t0, t1), x_iv(k, t0, t1))
                if k == KD - 1:
                    ss_mm(self.bank, ones_a, self.acc, True, True, after=self.finish)

            def finish(self):
                t0, t1 = self.tile * TT, (self.tile + 1) * TT
                sb = self.bank
                _, r = rstd_alloc()
                P.add("act", lambda e, o=r.ap, i=banks[sb][:, :]:
                      e.activation(out=o, in_=i, func=AF.Sqrt, bias=EPS),
                      reads=(bank_iv(sb),), writes=(r.iv(),))
                P.add("dve", lambda e, o=r.ap: e.reciprocal(o, o), reads=(r.iv(),), writes=(r.iv(),))
                hb = self.hb
                for k in range(KD):
                    g_ap, g_iv = gcol(self.gbase + k)
                    o = hb.ap[:, k * TT:(k + 1) * TT]
                    P.add("dve", lambda e, o=o, i=x_ap(k, t0, t1), g=g_ap, rr_=r.ap:
                          e.scalar_tensor_tensor(o, i, g, rr_, ALU.mult, ALU.mult),
                          reads=(x_iv(k, t0, t1), g_iv, r.iv()), writes=(hb.iv(k * TT, (k + 1) * TT),))
                self.done = True

            def drain(self):
                while self.squares_left():
                    self.square()
                    flush_sq()
                flush_sq()
                assert self.done

        hstate = {"key": None, "jobs": []}

        def prefetch_h(spec):
            tiles, gbase = spec
            flush_deferred_tiles(tiles)
            hstate["key"] = (tuple(tiles), gbase)
            hstate["jobs"] = [PJob(t, gbase, hbuf[i], i) for i, t in enumerate(tiles)]

        def request_h(spec):
            tiles, gbase = spec
            if hstate["key"] != (tuple(tiles), gbase):
                prefetch_h(spec)
            for jb in hstate["jobs"]:
                if not jb.done:
                    jb.drain()

        def advance_h(n):
            for jb in hstate["jobs"]:
                while n > 0 and jb.squares_left():
                    jb.square()
                    n -= 1

        class PostNorm:
            def __init__(self, tile, gbase, half, ysb, ssbank):
                self.tile, self.gbase, self.half, self.ysb, self.sb = tile, gbase, half, ysb, ssbank
                self.ones = ones_b if half else ones_a
                self.epsv = 4.0 * EPS if half else EPS
                self.acc = accb[ssbank - 6]

            def chunk(self, i, yb):
                ysb = self.ysb
                yo = ysb.ap[:, i * TT:(i + 1) * TT]
                act_op(yo, ysb.iv(i * TT, (i + 1) * TT), banks[yb][:, :], bank_iv(yb), AF.Copy)
                sq_accumulate(self.acc, i == 0, banks[yb][:, :], bank_iv(yb))
                if i == KD - 1:
                    ss_mm(self.sb, self.ones, self.acc, True, True, after=self.finish)

            def finish(self):
                t0, t1 = self.tile * TT, (self.tile + 1) * TT
                sb, ysb = self.sb, self.ysb
                ri, r = rstd_alloc()
                P.add("act", lambda e, o=r.ap, i=banks[sb][:, :], ev=self.epsv:
                      e.activation(out=o, in_=i, func=AF.Sqrt, bias=ev),
                      reads=(bank_iv(sb),), writes=(r.iv(),))

                def d_recip(r=r, ri=ri):
                    P.add("dve", lambda e, o=r.ap: e.reciprocal(o, o), reads=(r.iv(),), writes=(r.iv(),))
                    rstd_busy[ri] -= 1

                def d_upd(i, r=r, ri=ri):
                    g_ap, g_iv = gcol(self.gbase + i)
                    yo = ysb.ap[:, i * TT:(i + 1) * TT]
                    yiv = ysb.iv(i * TT, (i + 1) * TT)
                    P.add("dve", lambda e, o=yo, g=g_ap, rr_=r.ap:
                          e.scalar_tensor_tensor(o, o, g, rr_, ALU.mult, ALU.mult),
                          reads=(yiv, g_iv, r.iv()), writes=(yiv,))
                    P.add("dve", lambda e, o=x_ap(i, t0, t1), y=yo:
                          e.tensor_tensor(o, o, y, ALU.add),
                          reads=(x_iv(i, t0, t1), yiv), writes=(x_iv(i, t0, t1),))
                    rstd_busy[ri] -= 1

                rstd_busy[ri] += 1 + KD
                deferred.append((self.tile, d_recip))
                for i in range(KD):
                    deferred.append((self.tile, lambda i=i: d_upd(i)))

        def ffn(f, l, gpre, gpost, next_spec, on_first_half_final=None):
            NST = NT // NSUB
            for st in range(NST):
                tiles = [st * NSUB + u for u in range(NSUB)]
                request_h((tiles, gpre))
                for jp in range(NJP):
                    s = gu.acquire()
                    slot = gu_slots[s]
                    if jp == 4:
                        flush_deferred()
                    for j2 in range(2):
                        j = jp * 2 + j2
                        for sub in range(NSUB):
                            hb = hbuf[sub]
                            gb = rr["gu_bank"] % 2
                            ub = 2 + gb
                            rr["gu_bank"] += 1
                            for wi, bk in ((0, gb), (1, ub)):
                                for k in range(KD):
                                    e0 = wi * 2048 + k * 256 + j2 * 128
                                    mm(banks[bk][:, :], bank_iv(bk), slot.ap[:, e0:e0 + 128], slot.iv(e0, e0 + 128),
                                       hb.ap[:, k * TT:(k + 1) * TT], hb.iv(k * TT, (k + 1) * TT), k == 0, k == KD - 1)
                            sg = sgb[rr["sg"] % NSG]
                            rr["sg"] += 1
                            act_op(sg.ap, sg.iv(), banks[gb][:, :], bank_iv(gb), AF.Silu)
                            a0 = (j * NSUB + sub) * TT
                            P.add("dve", lambda e, o=actb.ap[:, a0:a0 + TT], u=banks[ub][:, :], g=sg.ap:
                                  e.tensor_tensor(o, u, g, ALU.mult),
                                  reads=(bank_iv(ub), sg.iv()), writes=(actb.iv(a0, a0 + TT),))
                            pump_deferred(2)
                    gu.release(s)
                nxt = ((([(st + 1) * NSUB + u for u in range(NSUB)]), gpre) if st + 1 < NST else next_spec)
                flush_deferred()
                if st == 1 and on_first_half_final is not None:
                    on_first_half_final()
                if nxt is not None:
                    prefetch_h(nxt)
                pns = [PostNorm(tiles[sub], gpost, True, ysb_f[sub], 6 + sub) for sub in range(NSUB)]
                for ip in range(4):
                    s = dn.acquire()
                    slot = dn_slots[s]
                    for sub in range(NSUB):
                        for i2 in range(2):
                            i = ip * 2 + i2
                            yb = 4 + (rr["y_bank"] % 2)
                            rr["y_bank"] += 1
                            for j in range(NJ):
                                e0 = j * 256 + i2 * 128
                                a0 = (j * NSUB + sub) * TT
                                mm(banks[yb][:, :], bank_iv(yb), slot.ap[:, e0:e0 + 128], slot.iv(e0, e0 + 128),
                                   actb.ap[:, a0:a0 + TT], actb.iv(a0, a0 + TT), j == 0, j == NJ - 1)
                            flush_sq()
                            pns[sub].chunk(i, yb)
                            if nxt is not None:
                                advance_h(2)
                    dn.release(s)
                flush_sq()

        def evac(out_ap, out_iv, bank_ap, biv):
            rr["ev"] += 1
            if rr["ev"] % 2:
                act_op(out_ap, out_iv, bank_ap, biv, AF.Copy)
            else:
                P.add("dve", lambda e, o=out_ap, i=bank_ap: e.tensor_copy(o, i), reads=(biv,), writes=(out_iv,))

        def mixer(l, gl, next_spec):
            gpre, gpost, gps = gl + G_MPRE, gl + G_MPOST, gl + G_PSCALE
            for st in range(NT // NSUB):
                tiles = [st * NSUB + u for u in range(NSUB)]
                request_h((tiles, gpre))
                for cp in range(2):
                    s = gu.acquire()
                    slot = gu_slots[s]
                    if cp == 1 and st == 0:
                        flush_deferred()
                        for (dst, srcd, sem) in ((fwb, fw_d, fw_sem), (pwb, pw_d, pw_sem)):
                            dma("pool", dst.ap.rearrange("p (h m) -> p h m", h=4),
                                srcd[l].rearrange("h c m -> c h m"), sem, writes=(dst.iv(),))
                    for sub in range(NSUB):
                        hb = hbuf[sub]
                        t0, t1 = tiles[sub] * TT, (tiles[sub] + 1) * TT
                        for c2 in range(4):
                            c = cp * 4 + c2
                            bk = rr["gu_bank"] % 4
                            rr["gu_bank"] += 1
                            for k in range(KD):
                                e0 = k * 512 + c2 * 128
                                mm(banks[bk][:, :], bank_iv(bk), slot.ap[:, e0:e0 + 128], slot.iv(e0, e0 + 128),
                                   hb.ap[:, k * TT:(k + 1) * TT], hb.iv(k * TT, (k + 1) * TT), k == 0, k == KD - 1)
                            if c < 4:
                                evac(uF.ap[:, c * S + t0:c * S + t1], uF.iv(c * S + t0, c * S + t1), banks[bk][:, :], bank_iv(bk))
                            else:
                                g = c - 4
                                e0 = g * UPW + PADP + t0
                                evac(uP.ap[:, e0:e0 + TT], uP.iv(e0, e0 + TT), banks[bk][:, :], bank_iv(bk))
                            pump_deferred(3)
                    gu.release(s)
            flush_deferred()
            up3 = uP.ap.rearrange("p (g w) -> p g w", g=4)
            P.add("dve", lambda e, o=up3[:, :, 0:PADP]: e.memset(o, 0.0),
                  writes=tuple(uP.iv(g_ * UPW, g_ * UPW + PADP) for g_ in range(4)))
            P.add("dve", lambda e, o=up3[:, :, PADP + S:UPW]: e.memset(o, 0.0),
                  writes=tuple(uP.iv(g_ * UPW + PADP + S, (g_ + 1) * UPW) for g_ in range(4)))
            dbufs = (diffs, diffs2)

            def pool_group(g):
                r = RADII[g]
                db = dbufs[g % 2]
                ub = g * UPW

                def u_(a, b):
                    return uP.ap[:, ub + a:ub + b], uP.iv(ub + a, ub + b)

                def tt_add(dst, a0, n, srcA, sa, srcB, sb_):
                    def ap_iv(bf, s0):
                        if bf is None:
                            return u_(s0, s0 + n)
                        return bf.ap[:, s0:s0 + n], bf.iv(s0, s0 + n)
                    o_ap, o_iv = ap_iv(dst, a0)
                    a_ap, a_iv = ap_iv(srcA, sa)
                    b_ap, b_iv = ap_iv(srcB, sb_)
                    P.add("dve", lambda e, o=o_ap, a=a_ap, b=b_ap: e.tensor_tensor(o, a, b, ALU.add),
                          reads=(a_iv, b_iv), writes=(o_iv,))

                tt_add(tmpA, 0, UPW - 1, None, 0, None, 1)
                wbuf = tmpA
                span = 2
                n = UPW - 1
                while span < 2 * r:
                    nb = tmpB if wbuf is tmpA else tmpA
                    n2 = n - span
                    tt_add(nb, 0, n2, wbuf, 0, wbuf, span)
                    wbuf, n, span = nb, n2, span * 2
                winb = tmpB if wbuf is tmpA else tmpA
                tt_add(winb, 0, S, wbuf, PADP - r, None, PADP + r)
                inv = 1.0 / (2 * r + 1)
                uc_ap, uc_iv = u_(PADP, PADP + S)
                P.add("dve", lambda e, o=db.ap, w=winb.ap[:, 0:S], u=uc_ap, iv_=inv:
                      e.scalar_tensor_tensor(o, w, iv_, u, ALU.mult, ALU.subtract),
                      reads=(winb.iv(0, S), uc_iv), writes=(db.iv(),))
                c0 = 2 * (r - 1)
                for (ts, cs) in ((0, c0), (S - r, c0 + r)):
                    ue_ap, ue_iv = u_(PADP + ts, PADP + ts + r)
                    P.add("dve", lambda e, o=edge.ap[:, 0:r], w=winb.ap[:, ts:ts + r], t_=invt.ap[:, cs:cs + r]:
                          e.tensor_tensor(o, w, t_, ALU.mult),
                          reads=(winb.iv(ts, ts + r), invt.iv()), writes=(edge.iv(),))
                    P.add("dve", lambda e, o=db.ap[:, ts:ts + r], a=edge.ap[:, 0:r], u=ue_ap:
                          e.tensor_tensor(o, a, u, ALU.subtract),
                          reads=(edge.iv(), ue_iv), writes=(db.iv(ts, ts + r),))

            def pw_group(g):
                db = dbufs[g % 2]
                sc_ap, sc_iv = gcol(gps + g)
                for kt in range(NT):
                    bk = 4 + (rr["y_bank"] % 2)
                    rr["y_bank"] += 1
                    mm(banks[bk][:, :], bank_iv(bk), pwb.ap[:, g * 128:(g + 1) * 128], pwb.iv(g * 128, (g + 1) * 128),
                       db.ap[:, kt * TT:(kt + 1) * TT], db.iv(kt * TT, (kt + 1) * TT), True, True)
                    e0 = g * S + kt * TT
                    act_op(ymixP.ap[:, e0:e0 + TT], ymixP.iv(e0, e0 + TT), banks[bk][:, :], bank_iv(bk),
                           AF.Copy, scale=sc_ap, extra_reads=(sc_iv,))

            def m3_head(hd):
                for sp2 in range(8):
                    bk = rr["gu_bank"] % 4
                    rr["gu_bank"] += 1
                    for s2 in range(2):
                        sc = sp2 * 2 + s2
                        e0 = hd * S + sc * 128
                        mm(banks[bk][:, s2 * 256:(s2 + 1) * 256], bank_iv(bk), uF.ap[:, e0:e0 + 128], uF.iv(e0, e0 + 128),
                           ccsc.ap, ccsc.iv(), True, True)
                    a0 = hd * 4096 + sp2 * 512
                    evac(Abuf.ap[:, a0:a0 + 512], Abuf.iv(a0, a0 + 512), banks[bk][:, :], bank_iv(bk))
                for sc in range(16):
                    a0 = hd * 4096 + sc * 256
                    mm(banks[hd][:, 0:32], bank_iv(hd), Abuf.ap[:, a0:a0 + 128], Abuf.iv(a0, a0 + 128),
                       ccsc.ap[:, 64:96], ccsc.iv(), sc == 0, sc == 15)
                e0 = hd * S + S // 2
                P.add("dve", lambda e, o=uF.ap[:, e0:e0 + 1], i=banks[hd][:, 0:1]:
                      e.tensor_scalar(o, i, 512.0, None, ALU.mult),
                      reads=(bank_iv(hd),), writes=(uF.iv(e0, e0 + 1),))

            qt, pt = sgb[0], rstdb[0]

            def m4_iter(kt, hp, base):
                for scp in range(8):
                    s = tw.acquire()
                    slot = tw_slots[s]
                    for h2 in range(2):
                        hd = hp * 2 + h2
                        pb, qb = base + h2, base + 2 + h2
                        for s2 in range(2):
                            sc = scp * 2 + s2
                            a0 = hd * 4096 + sc * 256
                            r0 = s2 * 1024
                            mm(banks[pb][:, :], bank_iv(pb), Abuf.ap[:, a0:a0 + 128], Abuf.iv(a0, a0 + 128),
                               slot.ap[:, r0:r0 + 512], slot.iv(r0, r0 + 512), sc == 0, sc == 15)
                            mm(banks[qb][:, :], bank_iv(qb), Abuf.ap[:, a0 + 128:a0 + 256], Abuf.iv(a0 + 128, a0 + 256),
                               slot.ap[:, r0 + 512:r0 + 1024], slot.iv(r0 + 512, r0 + 1024), sc == 0, sc == 15)
                    tw.release(s)
                for h2 in range(2):
                    hd = hp * 2 + h2
                    pb, qb = base + h2, base + 2 + h2
                    act_op(qt.ap, qt.iv(), banks[qb][:, :], bank_iv(qb), AF.Copy)
                    act_op(pt.ap, pt.iv(), banks[pb][:, :], bank_iv(pb), AF.Copy)
                    e0 = hd * S + kt * TT
                    P.add("dve", lambda e, o=uF.ap[:, e0:e0 + TT], p_=pt.ap, q_=qt.ap:
                          e.tensor_tensor(o, p_, q_, ALU.add),
                          reads=(pt.iv(), qt.iv()), writes=(uF.iv(e0, e0 + TT),))
                    j0 = 1 if kt == 0 else 0
                    n = TT - j0
                    khi = S - (kt * TT + j0)
                    col = hd * S + khi
                    uv = uF.ap[:, col:col + 1]
                    rev = bass.AP(uv.tensor, uv.offset, [list(uv.ap[0]), [-1, n]])
                    P.add("dve", lambda e, o=rev, p_=pt.ap[:, j0:TT], q_=qt.ap[:, j0:TT]:
                          e.tensor_tensor(o, p_, q_, ALU.subtract),
                          reads=(pt.iv(), qt.iv()), writes=(uF.iv(col - n + 1, col + 1),))

            pool_group(0)
            pw_group(0)
            pool_group(1)
            pw_group(1)
            m3_head(0)
            m3_head(1)
            pool_group(2)
            pool_group(3)
            dn.open_barrier()
            tw.push([mk_tw(kt, scp) for kt in range(2) for hp in range(2) for scp in range(8)])
            m4_iter(0, 0, 0)
            pw_group(2)
            pw_group(3)
            m3_head(2)
            m3_head(3)
            m4_iter(0, 1, 4)
            m4_iter(1, 0, 0)
            m4_iter(1, 1, 4)
            for tile in range(NT):
                t0, t1 = tile * TT, (tile + 1) * TT
                for hd in range(4):
                    bk = rr["gu_bank"] % 4
                    rr["gu_bank"] += 1
                    e0 = hd * S + t0
                    mm(banks[bk][:, :], bank_iv(bk), fwb.ap[:, hd * 128:(hd + 1) * 128], fwb.iv(hd * 128, (hd + 1) * 128),
                       uF.ap[:, e0:e0 + TT], uF.iv(e0, e0 + TT), True, True)
                    evac(ymixF.ap[:, hd * TT:(hd + 1) * TT], ymixF.iv(hd * TT, (hd + 1) * TT), banks[bk][:, :], bank_iv(bk))
                last = tile == NT - 1
                if last and next_spec is not None:
                    prefetch_h(next_spec)
                flush_deferred(keep=1 + KD)
                pn = PostNorm(tile, gpost, False, ysb_m[tile % 2], 6)
                for ip in range(2):
                    s = gu.acquire()
                    slot = gu_slots[s]
                    for i2 in range(4):
                        i = ip * 4 + i2
                        yb = 4 + (rr["y_bank"] % 2)
                        rr["y_bank"] += 1
                        for k in range(KD):
                            e0 = k * 512 + i2 * 128
                            if k < 4:
                                r_ap, r_iv = ymixF.ap[:, k * TT:(k + 1) * TT], ymixF.iv(k * TT, (k + 1) * TT)
                            else:
                                q0 = (k - 4) * S + t0
                                r_ap, r_iv = ymixP.ap[:, q0:q0 + TT], ymixP.iv(q0, q0 + TT)
                            mm(banks[yb][:, :], bank_iv(yb), slot.ap[:, e0:e0 + 128], slot.iv(e0, e0 + 128),
                               r_ap, r_iv, k == 0, k == KD - 1)
                        flush_sq()
                        pn.chunk(i, yb)
                        pump_deferred(3)
                        if last and next_spec is not None:
                            advance_h(2)
                    gu.release(s)
                flush_sq()

        misc_sems = {n: newsem(n) for n in ("gtab", "ccsc", "inv")}
        fw_sem = newsem("fw")
        pw_sem = newsem("pw")
        xsems = [[newsem("x%d_%d" % (k, hf)) for hf in range(2)] for k in range(KD)]
        osems = [[newsem("o%d_%d" % (k, hf)) for hf in range(2)] for k in range(KD)]
        HS = S // 2
        seq_state = {"q": 0}

        def load_half(q, hf):
            for k in range(KD):
                dma("sp", x_ap(k, hf * HS, (hf + 1) * HS), xT[q, k][:, hf * HS:(hf + 1) * HS], xsems[k][hf],
                    writes=(x_iv(k, hf * HS, (hf + 1) * HS),))

        def store_half(q, hf):
            for k in range(KD):
                dma("sp", outT[q, k][:, hf * HS:(hf + 1) * HS], x_ap(k, hf * HS, (hf + 1) * HS), osems[k][hf],
                    reads=(x_iv(k, hf * HS, (hf + 1) * HS),))

        def early_swap():
            q = seq_state["q"]
            store_half(q, 0)
            if q + 1 < NSEQ:
                load_half(q + 1, 0)

        dma("sp", gtab.ap, gtab_d, misc_sems["gtab"], writes=(gtab.iv(),))
        dma("sp", ccsc.ap, ccsc_d, misc_sems["ccsc"], writes=(ccsc.iv(),))
        dma("sp", invt.ap, inv_d, misc_sems["inv"], writes=(invt.iv(),))
        P.add("dve", lambda e: e.memset(ones_a.ap, 1.0 / D), writes=(ones_a.iv(),))
        P.add("dve", lambda e: e.memset(ones_b.ap, 4.0 / D), writes=(ones_b.iv(),))

        gu_pieces = []
        dn_pieces = []
        for q in range(NSEQ):
            for l in range(L):
                for f in range(2):
                    for st in range(NT // NSUB):
                        gu_pieces += [mk_gateup(f, l, jp) for jp in range(NJP)]
                        dn_pieces += [mk_down(f, l, ip) for ip in range(4)]
                    if f == 0:
                        dn_pieces.append(None)
                        for st in range(NT // NSUB):
                            gu_pieces += [mk_sq(w_in, l, cp) for cp in range(2)]
                        for tile in range(NT):
                            gu_pieces += [mk_sq(w_out, l, ip) for ip in range(2)]
        gu.push(gu_pieces)
        dn.push(dn_pieces)

        out_final = []
        load_half(0, 0)
        load_half(0, 1)
        for q in range(NSEQ):
            seq_state["q"] = q
            hstate["key"] = None
            for l in range(L):
                gl = l * G_PER_LAYER
                t01 = list(range(NSUB))
                ffn(0, l, gl + G_F1PRE, gl + G_F1POST, (t01, gl + G_MPRE))
                mixer(l, gl, (t01, gl + G_F2PRE))
                last_layer = l + 1 == L
                nxt = (t01, gl + G_PER_LAYER + G_F1PRE) if not last_layer else None
                ffn(1, l, gl + G_F2PRE, gl + G_F2POST, nxt, on_first_half_final=early_swap if last_layer else None)
            flush_deferred()
            store_half(q, 1)
            if q + 1 < NSEQ:
                load_half(q + 1, 1)
        out_final = [(osems[k][hf], dma_cnt[id(osems[k][hf])]) for k in range(KD) for hf in range(2)]

        P.finalize()
        with nc.Block() as block:
            @block.tensor
            def _(e):
                P.emit("pe", e, engsems)

            @block.scalar
            def _(e):
                P.emit("act", e, engsems)

            @block.vector
            def _(e):
                P.emit("dve", e, engsems)

            @block.gpsimd
            def _(e):
                P.emit("pool", e, engsems)

            @block.sync
            def _(e):
                P.emit("sp", e, engsems, extra_waits=out_final)
    return nc


def _const_tables():
    c = np.arange(128, dtype=np.float64)
    ang = 2.0 * np.pi * np.outer(c, c) / 128.0
    ccsc = np.concatenate([np.cos(ang) / 512.0, -np.sin(ang) / 512.0], axis=1)
    s = np.arange(S, dtype=np.int64)
    prod = np.outer(s, s) % S
    ang2 = 2.0 * np.pi * prod.astype(np.float64) / S
    C = np.cos(ang2).astype(np.float32)
    Sn = np.sin(ang2).astype(np.float32)
    tw = np.empty((2, 8, 128, 2, 1024), dtype=np.float32)
    Cr = C.reshape(8, 2, 128, NT, 512)
    Sr = Sn.reshape(8, 2, 128, NT, 512)
    tw[..., 0:512] = Cr.transpose(3, 0, 2, 1, 4)[:2]
    tw[..., 512:1024] = Sr.transpose(3, 0, 2, 1, 4)[:2]
    inv = np.zeros((128, 32), dtype=np.float32)
    for r in RADII:
        c0 = 2 * (r - 1)
        for t in range(r):
            cnt = t + r + 1
            inv[:, c0 + t] = 1.0 / cnt
            inv[:, c0 + r + t] = 1.0 / (2 * r - t)
    return (ccsc.astype(ml_dtypes.bfloat16), tw.astype(ml_dtypes.bfloat16), inv)


def _gtab(inputs, layers):
    cols = []
    for l in layers:
        for name in ("ffn1_pre_g", "ffn1_post_g", "mix_pre_g", "mix_post_g", "ffn2_pre_g", "ffn2_post_g"):
            cols.append(np.asarray(inputs[name][l], dtype=np.float32).reshape(KD, 128).T)
        cols.append(np.asarray(inputs["pool_scale"][l], dtype=np.float32).reshape(4, 128).T)
    return np.ascontiguousarray(np.concatenate(cols, axis=1))


_PROGS = {}


def _run(L, layers, xT_cores, inputs, consts):
    if L not in _PROGS:
        _PROGS[L] = build_program(L)
    nc = _PROGS[L]
    ccsc, tw, inv = consts
    sl = slice(layers[0], layers[-1] + 1)
    shared = {
        "wg1": np.ascontiguousarray(inputs["ffn1_w_gate"][sl]), "wu1": np.ascontiguousarray(inputs["ffn1_w_up"][sl]),
        "wd1": np.ascontiguousarray(inputs["ffn1_w_down"][sl]),
        "wg2": np.ascontiguousarray(inputs["ffn2_w_gate"][sl]), "wu2": np.ascontiguousarray(inputs["ffn2_w_up"][sl]),
        "wd2": np.ascontiguousarray(inputs["ffn2_w_down"][sl]),
        "w_in": np.ascontiguousarray(inputs["w_in"][sl]), "w_out": np.ascontiguousarray(inputs["w_out"][sl]),
        "fw": np.ascontiguousarray(inputs["fourier_w"][sl]), "pw": np.ascontiguousarray(inputs["pool_w"][sl]),
        "gtab": _gtab(inputs, layers), "ccsc": ccsc, "tw": tw, "invtab": inv,
    }
    in_maps = [dict(shared, xT=xT_cores[c]) for c in range(NCORES)]
    res = run_bass_kernel_spmd(nc, in_maps, core_ids=list(range(NCORES)))
    return [np.asarray(r["outT"]) for r in res.results]


N_LAUNCH_LAYERS = 4


def kernel(**inputs):
    inputs = {k: np.asarray(v) for k, v in inputs.items()}
    x = inputs["x"].astype(np.float32, copy=False)
    B = x.shape[0]
    Ltot = inputs["w_in"].shape[0]
    consts = _const_tables()
    xT_cores = [np.ascontiguousarray(x[c * NSEQ:(c + 1) * NSEQ].transpose(0, 2, 1)).reshape(NSEQ, KD, 128, S)
                for c in range(NCORES)]
    step = N_LAUNCH_LAYERS
    for l0 in range(0, Ltot, step):
        xT_cores = _run(step, list(range(l0, l0 + step)), xT_cores, inputs, consts)
    out = np.empty((B, S, D), dtype=np.float32)
    for c in range(NCORES):
        out[c * NSEQ:(c + 1) * NSEQ] = xT_cores[c].reshape(NSEQ, D, S).transpose(0, 2, 1)
    return out
```

```python
import contextlib
import numpy as np
import ml_dtypes
import concourse.bass as bass
import concourse.mybir as mybir
from concourse.bass_utils import run_bass_kernel_spmd

F32 = mybir.dt.float32
BF16 = mybir.dt.bfloat16
AF = mybir.ActivationFunctionType
ALU = mybir.AluOpType

D = 1024
FF = 2816
S = 2048
NCORES = 8
NSEQ = 2
KD = D // 128
NJ = FF // 128
NJP = NJ // 2
TT = 512
NSUB = 2
NSQ = 2
NSG = 1
NT = S // TT
EPS = 1e-6
PADP = 8
UPW = S + 2 * PADP
RADII = (1, 2, 4, 8)

G_F1PRE, G_F1POST, G_MPRE, G_MPOST, G_F2PRE, G_F2POST = 0, 8, 16, 24, 32, 40
G_PSCALE = 48
G_PER_LAYER = 52

GRAN = 256
PSUM_BASE = 1 << 20
SBUF_LIMIT = 212800


class Op:
    __slots__ = ("eng", "fn", "waits", "signal", "semval", "dsem", "dval", "is_dma")

    def __init__(self, eng, fn, is_dma):
        self.eng = eng
        self.fn = fn
        self.waits = []
        self.signal = False
        self.semval = 0
        self.dsem = None
        self.dval = 0
        self.is_dma = is_dma


class Prog:
    def __init__(self):
        self.ops = []
        self.lastw = {}
        self.readers = {}

    @staticmethod
    def _grans(iv):
        lo, hi = iv
        return range(lo // GRAN, (hi - 1) // GRAN + 1)

    def add(self, eng, fn, reads=(), writes=(), dsem=None, dval=0):
        idx = len(self.ops)
        is_dma = dsem is not None
        op = Op(eng, fn, is_dma)
        op.dsem, op.dval = dsem, dval
        raw = set()
        other = set()
        for iv in reads:
            for g in self._grans(iv):
                w = self.lastw.get(g)
                if w is not None:
                    raw.add(w)
        for iv in writes:
            for g in self._grans(iv):
                w = self.lastw.get(g)
                if w is not None:
                    other.add(w)
                rd = self.readers.get(g)
                if rd:
                    other.update(rd.values())
        other -= raw
        ops = self.ops
        for y in raw:
            oy = ops[y]
            if (not is_dma) and (not oy.is_dma) and oy.eng == eng and eng == "pe":
                continue
            op.waits.append(y)
        for y in other:
            oy = ops[y]
            if (not is_dma) and (not oy.is_dma) and oy.eng == eng and eng == "pe":
                continue
            op.waits.append(y)
        for y in op.waits:
            if not ops[y].is_dma:
                ops[y].signal = True
        key = ("d", idx) if is_dma else eng
        for iv in reads:
            for g in self._grans(iv):
                self.readers.setdefault(g, {})[key] = idx
        for iv in writes:
            for g in self._grans(iv):
                self.lastw[g] = idx
                self.readers[g] = {}
        self.ops.append(op)
        return idx

    def finalize(self):
        cnt = {}
        for op in self.ops:
            if op.is_dma:
                continue
            if op.signal:
                cnt[op.eng] = cnt.get(op.eng, 0) + 1
                op.semval = cnt[op.eng]

    def emit(self, eng_name, eng, engsems, extra_waits=()):
        waited = {}
        ops = self.ops
        for op in ops:
            if op.eng != eng_name:
                continue
            for y in op.waits:
                oy = ops[y]
                if oy.is_dma:
                    sem, val = oy.dsem, oy.dval
                else:
                    sem, val = engsems[oy.eng], oy.semval
                k = id(sem)
                if waited.get(k, 0) < val:
                    eng.wait_ge(sem, val)
                    waited[k] = val
            ins = op.fn(eng)
            if op.is_dma:
                ins.then_inc(op.dsem, 16)
            elif op.signal:
                ins.then_inc(engsems[eng_name], 1)
        for sem, val in extra_waits:
            eng.wait_ge(sem, val)


class Buf:
    def __init__(self, arena, off, nbytes, dtype):
        assert off % 4 == 0 and nbytes % 4 == 0
        self.off = off
        self.nbytes = nbytes
        self.dtype = dtype
        self.esz = 2 if dtype == BF16 else 4
        v = arena[:, off // 4:(off + nbytes) // 4]
        self.ap = v.bitcast(BF16) if dtype == BF16 else v
        self.n = nbytes // self.esz

    def iv(self, e0=0, e1=None):
        if e1 is None:
            e1 = self.n
        return (self.off + e0 * self.esz, self.off + e1 * self.esz)

    def sl(self, e0, e1):
        return self.ap[:, e0:e1]


def build_program(L):
    nc = bass.Bass("TRN2", target_bir_lowering=False)
    NG = G_PER_LAYER * L

    def din(name, shape, dt):
        return nc.dram_tensor(name, list(shape), dt, kind="ExternalInput").ap()

    xT = din("xT", [NSEQ, KD, 128, S], F32)
    wg = [din("wg1", [L, D, FF], F32), din("wg2", [L, D, FF], F32)]
    wu = [din("wu1", [L, D, FF], F32), din("wu2", [L, D, FF], F32)]
    wd = [din("wd1", [L, FF, D], F32), din("wd2", [L, FF, D], F32)]
    w_in = din("w_in", [L, D, D], F32)
    w_out = din("w_out", [L, D, D], F32)
    fw_d = din("fw", [L, 4, 128, 128], F32)
    pw_d = din("pw", [L, 4, 128, 128], F32)
    gtab_d = din("gtab", [128, NG], F32)
    ccsc_d = din("ccsc", [128, 256], BF16)
    tw_d = din("tw", [2, 8, 128, 2, 1024], BF16)
    inv_d = din("invtab", [128, 32], F32)
    outT = nc.dram_tensor("outT", [NSEQ, KD, 128, S], F32, kind="ExternalOutput").ap()

    P = Prog()
    es = contextlib.ExitStack()
    with es:
        arena = es.enter_context(nc.sbuf_tensor("arena", [128, SBUF_LIMIT // 4], F32))
        banks = [es.enter_context(nc.psum_tensor("bank%d" % b, [128, 512], F32)) for b in range(8)]

        def bank_iv(b):
            return (PSUM_BASE + b * 2048, PSUM_BASE + (b + 1) * 2048)

        nsem = [0]

        def newsem(name):
            nsem[0] += 1
            return es.enter_context(nc.semaphore(name))

        engsems = {e: newsem("sem_" + e) for e in ("pe", "act", "dve", "pool", "sp")}

        cur = [0]

        def alloc(nbytes, dtype):
            nb = (nbytes + GRAN - 1) // GRAN * GRAN
            b = Buf(arena, cur[0], nbytes, dtype)
            cur[0] += nb
            return b

        def at(off, nbytes, dtype):
            return Buf(arena, off, nbytes, dtype)

        xres = alloc(KD * S * 4, F32)
        SH = cur[0]
        hbuf = [at(SH + i * 8192, 8192, BF16) for i in range(2)]
        ACT_SZ = NJ * NSUB * TT * 2
        actb = at(SH + 16384, ACT_SZ, BF16)
        ysb_f = [at(SH + 16384 + ACT_SZ + i * KD * TT * 4, KD * TT * 4, F32) for i in range(NSUB)]
        FFN_END = SH + 16384 + ACT_SZ + NSUB * KD * TT * 4
        uF = at(SH + 16384, 4 * S * 2, BF16)
        UP0 = SH + 32768
        uP = at(UP0, 4 * UPW * 4, F32)
        UPSZ = (4 * UPW * 4 + GRAN - 1) // GRAN * GRAN
        Abuf = at(UP0, 4 * 16 * 256 * 2, BF16)
        ysb_m = [at(UP0 + i * KD * TT * 4, KD * TT * 4, F32) for i in range(2)]
        diffs = at(UP0 + UPSZ, S * 2, BF16)
        ymixF = at(UP0 + UPSZ, 4 * TT * 2, BF16)
        ymixP = at(UP0 + UPSZ + 4096, 4 * S * 2, BF16)
        MIX_END = UP0 + UPSZ + 4096 + 16384
        fwb = at(MIX_END, 4 * 128 * 2, BF16)
        pwb = at(MIX_END + 1024, 4 * 128 * 2, BF16)
        diffs2 = at(MIX_END + 2048, S * 2, BF16)
        assert MIX_END + 2048 + S * 2 <= FFN_END
        SH_END = max(FFN_END, MIX_END + 2048)
        twring_off = SH
        cur[0] = SH_END
        GU_SLOT = 8192
        DN_SLOT = NJ * 256 * 2
        NGU, NDN, NTW = 2, 2, 4
        gu_slots = [alloc(GU_SLOT, BF16) for _ in range(NGU)]
        dn_slots = [alloc(DN_SLOT, BF16) for _ in range(NDN)]
        tmpA = at(dn_slots[0].off, UPW * 4, F32)
        tmpB = at(dn_slots[0].off + UPW * 4, UPW * 4, F32)
        assert tmpB.off + tmpB.nbytes <= dn_slots[-1].off + DN_SLOT
        tw_slots = [at(twring_off + i * 4096, 4096, BF16) for i in range(NTW)]
        sgb = [alloc(TT * 4, F32) for _ in range(NSG)]
        sqb = [alloc(TT * 2, BF16) for _ in range(NSQ)]
        accb = [alloc(TT * 2, BF16) for _ in range(3)]
        rstdb = [alloc(TT * 4, F32) for _ in range(2)]
        edge = at(sgb[0].off, 64, F32)
        c0_ = cur[0]
        gtab = at(c0_, NG * 4, F32)
        c0_ += (NG * 4 + 31) // 32 * 32
        ones_a = at(c0_, 256, BF16)
        ones_b = at(c0_ + 256, 256, BF16)
        ccsc = at(c0_ + 512, 512, BF16)
        invt = at(c0_ + 1024, 128, F32)
        cur[0] = c0_ + 1152
        assert cur[0] <= SBUF_LIMIT, cur[0]

        def x_ap(k, t0, t1):
            return xres.ap[:, k * S + t0:k * S + t1]

        def x_iv(k, t0, t1):
            return xres.iv(k * S + t0, k * S + t1)

        dma_cnt = {}

        def dma(queue, out_ap, in_ap, sem, reads=(), writes=(), group_final=None):
            k = id(sem)
            if group_final is None:
                dma_cnt[k] = dma_cnt.get(k, 0) + 16
                val = dma_cnt[k]
            else:
                val = group_final
            if queue == "pool":
                fn = lambda e, o=out_ap, i=in_ap: e.dma_start(out=o, in_=i)
            else:
                fn = lambda e, o=out_ap, i=in_ap: e.dma_start(out=o, in_=i)
            return P.add(queue, fn, reads=reads, writes=writes, dsem=sem, dval=val)

        def dma_group(queue, sem, parts):
            k = id(sem)
            final = dma_cnt.get(k, 0) + 16 * len(parts)
            dma_cnt[k] = final
            for (o, i, r, w) in parts:
                dma(queue, o, i, sem, reads=r, writes=w, group_final=final)

        class Ring:
            def __init__(self, name, slots, queue):
                self.slots = slots
                self.queue = queue
                self.sems = [newsem("%s_%d" % (name, i)) for i in range(len(slots))]
                self.free = list(range(len(slots)))
                self.pending = []
                self.ready = []
                self.hold = False

            def push(self, pieces):
                self.pending.extend(pieces)
                self.pump()

            def open_barrier(self):
                assert self.pending and self.pending[0] is None
                self.pending.pop(0)
                self.pump()

            def pump(self):
                while self.free and self.pending:
                    if self.pending[0] is None:
                        break
                    s = self.free.pop(0)
                    mk = self.pending.pop(0)
                    dma_group(self.queue, self.sems[s], mk(self.slots[s]))
                    self.ready.append(s)

            def acquire(self):
                assert self.ready, "ring underflow"
                return self.ready.pop(0)

            def release(self, s):
                self.free.append(s)
                self.pump()

        gu = Ring("gu", gu_slots, "pool")
        dn = Ring("dn", dn_slots, "pool")
        tw = Ring("tw", tw_slots, "sp")

        def mk_gateup(f, l, jp):
            def mk(slot):
                parts = []
                for wi, w in enumerate((wg[f], wu[f])):
                    src = w[l, :, jp * 256:(jp + 1) * 256].rearrange("(k p) m -> p k m", p=128)
                    dst = slot.ap[:, wi * 2048:(wi + 1) * 2048].rearrange("p (k m) -> p k m", k=KD)
                    parts.append((dst, src, (), (slot.iv(wi * 2048, (wi + 1) * 2048),)))
                return parts
            return mk

        def mk_sq(wdram, l, cp):
            def mk(slot):
                src = wdram[l, :, cp * 512:(cp + 1) * 512].rearrange("(k p) m -> p k m", p=128)
                dst = slot.ap[:, 0:4096].rearrange("p (k m) -> p k m", k=KD)
                return [(dst, src, (), (slot.iv(0, 4096),))]
            return mk

        def mk_down(f, l, ip):
            def mk(slot):
                parts = []
                src = wd[f][l, :, ip * 256:(ip + 1) * 256].rearrange("(j p) m -> p j m", p=128)
                dst = slot.ap.rearrange("p (j m) -> p j m", j=NJ)
                for (j0, j1) in ((0, 11), (11, 22)):
                    parts.append((dst[:, j0:j1, :], src[:, j0:j1, :], (), (slot.iv(j0 * 256, j1 * 256),)))
                return parts
            return mk

        def mk_tw(kt, scp):
            def mk(slot):
                src = tw_d[kt, scp]
                dst = slot.ap.rearrange("p (a m) -> p a m", a=2)
                return [(dst, src, (), (slot.iv(),))]
            return mk

        rr = {"sq": 0, "sg": 0, "rstd": 0, "ss": 0, "gu_bank": 0, "y_bank": 0, "ev": 0}
        SS_BANKS = (6, 7)

        def mm(bank_ap, bank_ivs, lhsT, l_iv, rhs, r_iv, start, stop):
            P.add("pe", lambda e, o=bank_ap, a=lhsT, b=rhs, s0=start, s1=stop:
                  e.matmul(o, a, b, start=s0, stop=s1),
                  reads=(l_iv, r_iv), writes=(bank_ivs,))

        def act_op(out_ap, out_iv, in_ap, in_iv, func, scale=None, extra_reads=()):
            if scale is None:
                fn = lambda e, o=out_ap, i=in_ap, f=func: e.activation(out=o, in_=i, func=f)
            else:
                fn = lambda e, o=out_ap, i=in_ap, f=func, s=scale: e.activation(out=o, in_=i, func=f, scale=s)
            P.add("act", fn, reads=(in_iv,) + tuple(extra_reads), writes=(out_iv,))

        def gcol(c):
            return gtab.ap[:, c:c + 1], gtab.iv(c, c + 1)

        sq_pending = []
        deferred = []
        rstd_busy = [0, 0]

        def pump_deferred(n):
            while n > 0 and deferred:
                deferred.pop(0)[1]()
                n -= 1

        def flush_deferred(keep=0):
            while len(deferred) > keep:
                deferred.pop(0)[1]()

        def flush_deferred_tiles(tiles):
            last = -1
            for n_, (t_, _) in enumerate(deferred):
                if t_ in tiles:
                    last = n_
            for _ in range(last + 1):
                deferred.pop(0)[1]()

        def rstd_alloc():
            i = rr["rstd"] % 2
            rr["rstd"] += 1
            if rstd_busy[i]:
                flush_deferred()
            assert rstd_busy[i] == 0
            return i, rstdb[i]

        def sq_alloc():
            q = sqb[rr["sq"] % NSQ]
            rr["sq"] += 1
            return q

        def sq_accumulate(acc, first, in_ap, in_iv):
            if first:
                act_op(acc.ap, acc.iv(), in_ap, in_iv, AF.Square)
            else:
                q = sq_alloc()
                act_op(q.ap, q.iv(), in_ap, in_iv, AF.Square)
                P.add("dve", lambda e, o=acc.ap, b=q.ap: e.tensor_tensor(o, o, b, ALU.add),
                      reads=(acc.iv(), q.iv()), writes=(acc.iv(),))

        def flush_sq():
            while sq_pending:
                sq_pending.pop(0)()

        def ss_mm(bank, ones, q, start, stop, after=None):
            def reg():
                mm(banks[bank][:, :], bank_iv(bank), ones.ap, ones.iv(), q.ap, q.iv(), start, stop)
                if after is not None:
                    after()
            sq_pending.append(reg)

        class PJob:
            def __init__(self, tile, gbase, hb, bank):
                self.tile, self.gbase, self.hb, self.bank = tile, gbase, hb, bank
                self.acc = accb[2]
                self.nsq = 0
                self.done = False

            def squares_left(self):
                return KD - self.nsq

            def square(self):
                t0, t1 = self.tile * TT, (self.tile + 1) * TT
                k = self.nsq
                self.nsq += 1
                if k == 0:
                    flush_sq()
                sq_accumulate(self.acc, k == 0, x_ap(k, t0, t1), x_iv(k, t0, t1))
                if k == KD - 1:
                    ss_mm(self.bank, ones_a, self.acc, True, True, after=self.finish)

            def finish(self):
                t0, t1 = self.tile * TT, (self.tile + 1) * TT
                sb = self.bank
                _, r = rstd_alloc()
                P.add("act", lambda e, o=r.ap, i=banks[sb][:, :]:
                      e.activation(out=o, in_=i, func=AF.Sqrt, bias=EPS),
                      reads=(bank_iv(sb),), writes=(r.iv(),))
                P.add("dve", lambda e, o=r.ap: e.reciprocal(o, o), reads=(r.iv(),), writes=(r.iv(),))
                hb = self.hb
                for k in range(KD):
                    g_ap, g_iv = gcol(self.gbase + k)
                    o = hb.ap[:, k * TT:(k + 1) * TT]
                    P.add("dve", lambda e, o=o, i=x_ap(k, t0, t1), g=g_ap, rr_=r.ap:
                          e.scalar_tensor_tensor(o, i, g, rr_, ALU.mult, ALU.mult),
                          reads=(x_iv(k, t0, t1), g_iv, r.iv()), writes=(hb.iv(k * TT, (k + 1) * TT),))
                self.done = True

            def drain(self):
                while self.squares_left():
                    self.square()
                    flush_sq()
                flush_sq()
                assert self.done

        hstate = {"key": None, "jobs": []}

        def prefetch_h(spec):
            tiles, gbase = spec
            flush_deferred_tiles(tiles)
            hstate["key"] = (tuple(tiles), gbase)
            hstate["jobs"] = [PJob(t, gbase, hbuf[i], i) for i, t in enumerate(tiles)]

        def request_h(spec):
            tiles, gbase = spec
            if hstate["key"] != (tuple(tiles), gbase):
                prefetch_h(spec)
            for jb in hstate["jobs"]:
                if not jb.done:
                    jb.drain()

        def advance_h(n):
            for jb in hstate["jobs"]:
                while n > 0 and jb.squares_left():
                    jb.square()
                    n -= 1

        class PostNorm:
            def __init__(self, tile, gbase, half, ysb, ssbank):
                self.tile, self.gbase, self.half, self.ysb, self.sb = tile, gbase, half, ysb, ssbank
                self.ones = ones_b if half else ones_a
                self.epsv = 4.0 * EPS if half else EPS
                self.acc = accb[ssbank - 6]

            def chunk(self, i, yb):
                ysb = self.ysb
                yo = ysb.ap[:, i * TT:(i + 1) * TT]
                act_op(yo, ysb.iv(i * TT, (i + 1) * TT), banks[yb][:, :], bank_iv(yb), AF.Copy)
                sq_accumulate(self.acc, i == 0, banks[yb][:, :], bank_iv(yb))
                if i == KD - 1:
                    ss_mm(self.sb, self.ones, self.acc, True, True, after=self.finish)

            def finish(self):
                t0, t1 = self.tile * TT, (self.tile + 1) * TT
                sb, ysb = self.sb, self.ysb
                ri, r = rstd_alloc()
                P.add("act", lambda e, o=r.ap, i=banks[sb][:, :], ev=self.epsv:
                      e.activation(out=o, in_=i, func=AF.Sqrt, bias=ev),
                      reads=(bank_iv(sb),), writes=(r.iv(),))

                def d_recip(r=r, ri=ri):
                    P.add("dve", lambda e, o=r.ap: e.reciprocal(o, o), reads=(r.iv(),), writes=(r.iv(),))
                    rstd_busy[ri] -= 1

                def d_upd(i, r=r, ri=ri):
                    g_ap, g_iv = gcol(self.gbase + i)
                    yo = ysb.ap[:, i * TT:(i + 1) * TT]
                    yiv = ysb.iv(i * TT, (i + 1) * TT)
                    P.add("dve", lambda e, o=yo, g=g_ap, rr_=r.ap:
                          e.scalar_tensor_tensor(o, o, g, rr_, ALU.mult, ALU.mult),
                          reads=(yiv, g_iv, r.iv()), writes=(yiv,))
                    P.add("dve", lambda e, o=x_ap(i, t0, t1), y=yo:
                          e.tensor_tensor(o, o, y, ALU.add),
                          reads=(x_iv(i, t0, t1), yiv), writes=(x_iv(i, t0, t1),))
                    rstd_busy[ri] -= 1

                rstd_busy[ri] += 1 + KD
                deferred.append((self.tile, d_recip))
                for i in range(KD):
                    deferred.append((self.tile, lambda i=i: d_upd(i)))

        def ffn(f, l, gpre, gpost, next_spec, on_first_half_final=None):
            NST = NT // NSUB
            for st in range(NST):
                tiles = [st * NSUB + u for u in range(NSUB)]
                request_h((tiles, gpre))
                for jp in range(NJP):
                    s = gu.acquire()
                    slot = gu_slots[s]
                    if jp == 4:
                        flush_deferred()
                    for j2 in range(2):
                        j = jp * 2 + j2
                        for sub in range(NSUB):
                            hb = hbuf[sub]
                            gb = rr["gu_bank"] % 2
                            ub = 2 + gb
                            rr["gu_bank"] += 1
                            for wi, bk in ((0, gb), (1, ub)):
                                for k in range(KD):
                                    e0 = wi * 2048 + k * 256 + j2 * 128
                                    mm(banks[bk][:, :], bank_iv(bk), slot.ap[:, e0:e0 + 128], slot.iv(e0, e0 + 128),
                                       hb.ap[:, k * TT:(k + 1) * TT], hb.iv(k * TT, (k + 1) * TT), k == 0, k == KD - 1)
                            sg = sgb[rr["sg"] % NSG]
                            rr["sg"] += 1
                            act_op(sg.ap, sg.iv(), banks[gb][:, :], bank_iv(gb), AF.Silu)
                            a0 = (j * NSUB + sub) * TT
                            P.add("dve", lambda e, o=actb.ap[:, a0:a0 + TT], u=banks[ub][:, :], g=sg.ap:
                                  e.tensor_tensor(o, u, g, ALU.mult),
                                  reads=(bank_iv(ub), sg.iv()), writes=(actb.iv(a0, a0 + TT),))
                            pump_deferred(2)
                    gu.release(s)
                nxt = ((([(st + 1) * NSUB + u for u in range(NSUB)]), gpre) if st + 1 < NST else next_spec)
                flush_deferred()
                if st == 1 and on_first_half_final is not None:
                    on_first_half_final()
                if nxt is not None:
                    prefetch_h(nxt)
                pns = [PostNorm(tiles[sub], gpost, True, ysb_f[sub], 6 + sub) for sub in range(NSUB)]
                for ip in range(4):
                    s = dn.acquire()
                    slot = dn_slots[s]
                    for sub in range(NSUB):
                        for i2 in range(2):
                            i = ip * 2 + i2
                            yb = 4 + (rr["y_bank"] % 2)
                            rr["y_bank"] += 1
                            for j in range(NJ):
                                e0 = j * 256 + i2 * 128
                                a0 = (j * NSUB + sub) * TT
                                mm(banks[yb][:, :], bank_iv(yb), slot.ap[:, e0:e0 + 128], slot.iv(e0, e0 + 128),
                                   actb.ap[:, a0:a0 + TT], actb.iv(a0, a0 + TT), j == 0, j == NJ - 1)
                            flush_sq()
                            pns[sub].chunk(i, yb)
                            if nxt is not None:
                                advance_h(2)
                    dn.release(s)
                flush_sq()

        def evac(out_ap, out_iv, bank_ap, biv, act_only=False):
            rr["ev"] += 1
            if act_only or rr["ev"] % 2:
                act_op(out_ap, out_iv, bank_ap, biv, AF.Copy)
            else:
                P.add("dve", lambda e, o=out_ap, i=bank_ap: e.tensor_copy(o, i), reads=(biv,), writes=(out_iv,))

        def mixer(l, gl, next_spec):
            gpre, gpost, gps = gl + G_MPRE, gl + G_MPOST, gl + G_PSCALE
            for st in range(NT // NSUB):
                tiles = [st * NSUB + u for u in range(NSUB)]
                request_h((tiles, gpre))
                for cp in range(2):
                    s = gu.acquire()
                    slot = gu_slots[s]
                    if cp == 1 and st == 0:
                        flush_deferred()
                        for (dst, srcd, sem) in ((fwb, fw_d, fw_sem), (pwb, pw_d, pw_sem)):
                            dma("pool", dst.ap.rearrange("p (h m) -> p h m", h=4),
                                srcd[l].rearrange("h c m -> c h m"), sem, writes=(dst.iv(),))
                    for sub in range(NSUB):
                        hb = hbuf[sub]
                        t0, t1 = tiles[sub] * TT, (tiles[sub] + 1) * TT
                        for c2 in range(4):
                            c = cp * 4 + c2
                            bk = rr["gu_bank"] % 4
                            rr["gu_bank"] += 1
                            for k in range(KD):
                                e0 = k * 512 + c2 * 128
                                mm(banks[bk][:, :], bank_iv(bk), slot.ap[:, e0:e0 + 128], slot.iv(e0, e0 + 128),
                                   hb.ap[:, k * TT:(k + 1) * TT], hb.iv(k * TT, (k + 1) * TT), k == 0, k == KD - 1)
                            if c < 4:
                                evac(uF.ap[:, c * S + t0:c * S + t1], uF.iv(c * S + t0, c * S + t1), banks[bk][:, :], bank_iv(bk))
                            else:
                                g = c - 4
                                e0 = g * UPW + PADP + t0
                                evac(uP.ap[:, e0:e0 + TT], uP.iv(e0, e0 + TT), banks[bk][:, :], bank_iv(bk))
                            pump_deferred(3)
                    gu.release(s)
            flush_deferred()
            up3 = uP.ap.rearrange("p (g w) -> p g w", g=4)
            P.add("dve", lambda e, o=up3[:, :, 0:PADP]: e.memset(o, 0.0),
                  writes=tuple(uP.iv(g_ * UPW, g_ * UPW + PADP) for g_ in range(4)))
            P.add("dve", lambda e, o=up3[:, :, PADP + S:UPW]: e.memset(o, 0.0),
                  writes=tuple(uP.iv(g_ * UPW + PADP + S, (g_ + 1) * UPW) for g_ in range(4)))
            dbufs = (diffs, diffs2)

            def pool_group(g):
                r = RADII[g]
                db = dbufs[g % 2]
                ub = g * UPW

                def u_(a, b):
                    return uP.ap[:, ub + a:ub + b], uP.iv(ub + a, ub + b)

                def tt_add(dst, a0, n, srcA, sa, srcB, sb_):
                    def ap_iv(bf, s0):
                        if bf is None:
                            return u_(s0, s0 + n)
                        return bf.ap[:, s0:s0 + n], bf.iv(s0, s0 + n)
                    o_ap, o_iv = ap_iv(dst, a0)
                    a_ap, a_iv = ap_iv(srcA, sa)
                    b_ap, b_iv = ap_iv(srcB, sb_)
                    P.add("dve", lambda e, o=o_ap, a=a_ap, b=b_ap: e.tensor_tensor(o, a, b, ALU.add),
                          reads=(a_iv, b_iv), writes=(o_iv,))

                tt_add(tmpA, 0, UPW - 1, None, 0, None, 1)
                wbuf = tmpA
                span = 2
                n = UPW - 1
                while span < 2 * r:
                    nb = tmpB if wbuf is tmpA else tmpA
                    n2 = n - span
                    tt_add(nb, 0, n2, wbuf, 0, wbuf, span)
                    wbuf, n, span = nb, n2, span * 2
                winb = tmpB if wbuf is tmpA else tmpA
                tt_add(winb, 0, S, wbuf, PADP - r, None, PADP + r)
                inv = 1.0 / (2 * r + 1)
                uc_ap, uc_iv = u_(PADP, PADP + S)
                P.add("dve", lambda e, o=db.ap, w=winb.ap[:, 0:S], u=uc_ap, iv_=inv:
                      e.scalar_tensor_tensor(o, w, iv_, u, ALU.mult, ALU.subtract),
                      reads=(winb.iv(0, S), uc_iv), writes=(db.iv(),))
                c0 = 2 * (r - 1)
                for (ts, cs) in ((0, c0), (S - r, c0 + r)):
                    ue_ap, ue_iv = u_(PADP + ts, PADP + ts + r)
                    P.add("dve", lambda e, o=edge.ap[:, 0:r], w=winb.ap[:, ts:ts + r], t_=invt.ap[:, cs:cs + r]:
                          e.tensor_tensor(o, w, t_, ALU.mult),
                          reads=(winb.iv(ts, ts + r), invt.iv()), writes=(edge.iv(),))
                    P.add("dve", lambda e, o=db.ap[:, ts:ts + r], a=edge.ap[:, 0:r], u=ue_ap:
                          e.tensor_tensor(o, a, u, ALU.subtract),
                          reads=(edge.iv(), ue_iv), writes=(db.iv(ts, ts + r),))

            def pw_group(g):
                db = dbufs[g % 2]
                sc_ap, sc_iv = gcol(gps + g)
                for kt in range(NT):
                    bk = 4 + (rr["y_bank"] % 2)
                    rr["y_bank"] += 1
                    mm(banks[bk][:, :], bank_iv(bk), pwb.ap[:, g * 128:(g + 1) * 128], pwb.iv(g * 128, (g + 1) * 128),
                       db.ap[:, kt * TT:(kt + 1) * TT], db.iv(kt * TT, (kt + 1) * TT), True, True)
                    e0 = g * S + kt * TT
                    act_op(ymixP.ap[:, e0:e0 + TT], ymixP.iv(e0, e0 + TT), banks[bk][:, :], bank_iv(bk),
                           AF.Copy, scale=sc_ap, extra_reads=(sc_iv,))

            def m3_head(hd):
                for sp2 in range(8):
                    bk = rr["gu_bank"] % 4
                    rr["gu_bank"] += 1
                    for s2 in range(2):
                        sc = sp2 * 2 + s2
                        e0 = hd * S + sc * 128
                        mm(banks[bk][:, s2 * 256:(s2 + 1) * 256], bank_iv(bk), uF.ap[:, e0:e0 + 128], uF.iv(e0, e0 + 128),
                           ccsc.ap, ccsc.iv(), True, True)
                    a0 = hd * 4096 + sp2 * 512
                    evac(Abuf.ap[:, a0:a0 + 512], Abuf.iv(a0, a0 + 512), banks[bk][:, :], bank_iv(bk),
                         act_only=(hd < 2))
                for sc in range(16):
                    a0 = hd * 4096 + sc * 256
                    mm(banks[hd][:, 0:32], bank_iv(hd), Abuf.ap[:, a0:a0 + 128], Abuf.iv(a0, a0 + 128),
                       ccsc.ap[:, 64:96], ccsc.iv(), sc == 0, sc == 15)
                e0 = hd * S + S // 2
                P.add("dve", lambda e, o=uF.ap[:, e0:e0 + 1], i=banks[hd][:, 0:1]:
                      e.tensor_scalar(o, i, 512.0, None, ALU.mult),
                      reads=(bank_iv(hd),), writes=(uF.iv(e0, e0 + 1),))

            qt, pt = sgb[0], rstdb[0]

            def m4_iter(kt, hp, base):
                for scp in range(8):
                    s = tw.acquire()
                    slot = tw_slots[s]
                    for h2 in range(2):
                        hd = hp * 2 + h2
                        pb, qb = base + h2, base + 2 + h2
                        for s2 in range(2):
                            sc = scp * 2 + s2
                            a0 = hd * 4096 + sc * 256
                            r0 = s2 * 1024
                            mm(banks[pb][:, :], bank_iv(pb), Abuf.ap[:, a0:a0 + 128], Abuf.iv(a0, a0 + 128),
                               slot.ap[:, r0:r0 + 512], slot.iv(r0, r0 + 512), sc == 0, sc == 15)
                            mm(banks[qb][:, :], bank_iv(qb), Abuf.ap[:, a0 + 128:a0 + 256], Abuf.iv(a0 + 128, a0 + 256),
                               slot.ap[:, r0 + 512:r0 + 1024], slot.iv(r0 + 512, r0 + 1024), sc == 0, sc == 15)
                    tw.release(s)
                for h2 in range(2):
                    hd = hp * 2 + h2
                    pb, qb = base + h2, base + 2 + h2
                    act_op(qt.ap, qt.iv(), banks[qb][:, :], bank_iv(qb), AF.Copy)
                    act_op(pt.ap, pt.iv(), banks[pb][:, :], bank_iv(pb), AF.Copy)
                    e0 = hd * S + kt * TT
                    P.add("dve", lambda e, o=uF.ap[:, e0:e0 + TT], p_=pt.ap, q_=qt.ap:
                          e.tensor_tensor(o, p_, q_, ALU.add),
                          reads=(pt.iv(), qt.iv()), writes=(uF.iv(e0, e0 + TT),))
                    j0 = 1 if kt == 0 else 0
                    n = TT - j0
                    khi = S - (kt * TT + j0)
                    col = hd * S + khi
                    uv = uF.ap[:, col:col + 1]
                    rev = bass.AP(uv.tensor, uv.offset, [list(uv.ap[0]), [-1, n]])
                    P.add("dve", lambda e, o=rev, p_=pt.ap[:, j0:TT], q_=qt.ap[:, j0:TT]:
                          e.tensor_tensor(o, p_, q_, ALU.subtract),
                          reads=(pt.iv(), qt.iv()), writes=(uF.iv(col - n + 1, col + 1),))

            pool_group(0)
            pw_group(0)
            pool_group(1)
            pw_group(1)
            m3_head(0)
            m3_head(1)
            pool_group(2)
            pool_group(3)
            dn.open_barrier()
            tw.push([mk_tw(kt, scp) for kt in range(2) for hp in range(2) for scp in range(8)])
            m4_iter(0, 0, 0)
            pw_group(2)
            pw_group(3)
            m3_head(2)
            m3_head(3)
            m4_iter(0, 1, 4)
            m4_iter(1, 0, 0)
            m4_iter(1, 1, 4)
            for tile in range(NT):
                t0, t1 = tile * TT, (tile + 1) * TT
                for hd in range(4):
                    bk = rr["gu_bank"] % 4
                    rr["gu_bank"] += 1
                    e0 = hd * S + t0
                    mm(banks[bk][:, :], bank_iv(bk), fwb.ap[:, hd * 128:(hd + 1) * 128], fwb.iv(hd * 128, (hd + 1) * 128),
                       uF.ap[:, e0:e0 + TT], uF.iv(e0, e0 + TT), True, True)
                    evac(ymixF.ap[:, hd * TT:(hd + 1) * TT], ymixF.iv(hd * TT, (hd + 1) * TT), banks[bk][:, :], bank_iv(bk))
                last = tile == NT - 1
                if last and next_spec is not None:
                    prefetch_h(next_spec)
                flush_deferred(keep=1 + KD)
                pn = PostNorm(tile, gpost, False, ysb_m[tile % 2], 6)
                for ip in range(2):
                    s = gu.acquire()
                    slot = gu_slots[s]
                    for i2 in range(4):
                        i = ip * 4 + i2
                        yb = 4 + (rr["y_bank"] % 2)
                        rr["y_bank"] += 1
                        for k in range(KD):
                            e0 = k * 512 + i2 * 128
                            if k < 4:
                                r_ap, r_iv = ymixF.ap[:, k * TT:(k + 1) * TT], ymixF.iv(k * TT, (k + 1) * TT)
                            else:
                                q0 = (k - 4) * S + t0
                                r_ap, r_iv = ymixP.ap[:, q0:q0 + TT], ymixP.iv(q0, q0 + TT)
                            mm(banks[yb][:, :], bank_iv(yb), slot.ap[:, e0:e0 + 128], slot.iv(e0, e0 + 128),
                               r_ap, r_iv, k == 0, k == KD - 1)
                        flush_sq()
                        pn.chunk(i, yb)
                        pump_deferred(3)
                        if last and next_spec is not None:
                            advance_h(2)
                    gu.release(s)
                flush_sq()

        misc_sems = {n: newsem(n) for n in ("gtab", "ccsc", "inv")}
        fw_sem = newsem("fw")
        pw_sem = newsem("pw")
        xsems = [[newsem("x%d_%d" % (k, hf)) for hf in range(2)] for k in range(KD)]
        osems = [[newsem("o%d_%d" % (k, hf)) for hf in range(2)] for k in range(KD)]
        HS = S // 2
        seq_state = {"q": 0}

        def load_half(q, hf):
            for k in range(KD):
                dma("sp", x_ap(k, hf * HS, (hf + 1) * HS), xT[q, k][:, hf * HS:(hf + 1) * HS], xsems[k][hf],
                    writes=(x_iv(k, hf * HS, (hf + 1) * HS),))

        def store_half(q, hf):
            for k in range(KD):
                dma("sp", outT[q, k][:, hf * HS:(hf + 1) * HS], x_ap(k, hf * HS, (hf + 1) * HS), osems[k][hf],
                    reads=(x_iv(k, hf * HS, (hf + 1) * HS),))

        def early_swap():
            q = seq_state["q"]
            store_half(q, 0)
            if q + 1 < NSEQ:
                load_half(q + 1, 0)

        dma("sp", gtab.ap, gtab_d, misc_sems["gtab"], writes=(gtab.iv(),))
        dma("sp", ccsc.ap, ccsc_d, misc_sems["ccsc"], writes=(ccsc.iv(),))
        dma("sp", invt.ap, inv_d, misc_sems["inv"], writes=(invt.iv(),))
        P.add("dve", lambda e: e.memset(ones_a.ap, 1.0 / D), writes=(ones_a.iv(),))
        P.add("dve", lambda e: e.memset(ones_b.ap, 4.0 / D), writes=(ones_b.iv(),))

        gu_pieces = []
        dn_pieces = []
        for q in range(NSEQ):
            for l in range(L):
                for f in range(2):
                    for st in range(NT // NSUB):
                        gu_pieces += [mk_gateup(f, l, jp) for jp in range(NJP)]
                        dn_pieces += [mk_down(f, l, ip) for ip in range(4)]
                    if f == 0:
                        dn_pieces.append(None)
                        for st in range(NT // NSUB):
                            gu_pieces += [mk_sq(w_in, l, cp) for cp in range(2)]
                        for tile in range(NT):
                            gu_pieces += [mk_sq(w_out, l, ip) for ip in range(2)]
        gu.push(gu_pieces)
        dn.push(dn_pieces)

        out_final = []
        load_half(0, 0)
        load_half(0, 1)
        for q in range(NSEQ):
            seq_state["q"] = q
            hstate["key"] = None
            for l in range(L):
                gl = l * G_PER_LAYER
                t01 = list(range(NSUB))
                ffn(0, l, gl + G_F1PRE, gl + G_F1POST, (t01, gl + G_MPRE))
                mixer(l, gl, (t01, gl + G_F2PRE))
                last_layer = l + 1 == L
                nxt = (t01, gl + G_PER_LAYER + G_F1PRE) if not last_layer else None
                ffn(1, l, gl + G_F2PRE, gl + G_F2POST, nxt, on_first_half_final=early_swap if last_layer else None)
            flush_deferred()
            store_half(q, 1)
            if q + 1 < NSEQ:
                load_half(q + 1, 1)
        out_final = [(osems[k][hf], dma_cnt[id(osems[k][hf])]) for k in range(KD) for hf in range(2)]

        P.finalize()
        with nc.Block() as block:
            @block.tensor
            def _(e):
                P.emit("pe", e, engsems)

            @block.scalar
            def _(e):
                P.emit("act", e, engsems)

            @block.vector
            def _(e):
                P.emit("dve", e, engsems)

            @block.gpsimd
            def _(e):
                P.emit("pool", e, engsems)

            @block.sync
            def _(e):
                P.emit("sp", e, engsems, extra_waits=out_final)
    return nc


def _const_tables():
    c = np.arange(128, dtype=np.float64)
    ang = 2.0 * np.pi * np.outer(c, c) / 128.0
    ccsc = np.concatenate([np.cos(ang) / 512.0, -np.sin(ang) / 512.0], axis=1)
    s = np.arange(S, dtype=np.int64)
    prod = np.outer(s, s) % S
    ang2 = 2.0 * np.pi * prod.astype(np.float64) / S
    C = np.cos(ang2).astype(np.float32)
    Sn = np.sin(ang2).astype(np.float32)
    tw = np.empty((2, 8, 128, 2, 1024), dtype=np.float32)
    Cr = C.reshape(8, 2, 128, NT, 512)
    Sr = Sn.reshape(8, 2, 128, NT, 512)
    tw[..., 0:512] = Cr.transpose(3, 0, 2, 1, 4)[:2]
    tw[..., 512:1024] = Sr.transpose(3, 0, 2, 1, 4)[:2]
    inv = np.zeros((128, 32), dtype=np.float32)
    for r in RADII:
        c0 = 2 * (r - 1)
        for t in range(r):
            cnt = t + r + 1
            inv[:, c0 + t] = 1.0 / cnt
            inv[:, c0 + r + t] = 1.0 / (2 * r - t)
    return (ccsc.astype(ml_dtypes.bfloat16), tw.astype(ml_dtypes.bfloat16), inv)


def _gtab(inputs, layers):
    cols = []
    for l in layers:
        for name in ("ffn1_pre_g", "ffn1_post_g", "mix_pre_g", "mix_post_g", "ffn2_pre_g", "ffn2_post_g"):
            cols.append(np.asarray(inputs[name][l], dtype=np.float32).reshape(KD, 128).T)
        cols.append(np.asarray(inputs["pool_scale"][l], dtype=np.float32).reshape(4, 128).T)
    return np.ascontiguousarray(np.concatenate(cols, axis=1))


_PROGS = {}


def _run(L, layers, xT_cores, inputs, consts):
    if L not in _PROGS:
        _PROGS[L] = build_program(L)
    nc = _PROGS[L]
    ccsc, tw, inv = consts
    sl = slice(layers[0], layers[-1] + 1)
    shared = {
        "wg1": np.ascontiguousarray(inputs["ffn1_w_gate"][sl]), "wu1": np.ascontiguousarray(inputs["ffn1_w_up"][sl]),
        "wd1": np.ascontiguousarray(inputs["ffn1_w_down"][sl]),
        "wg2": np.ascontiguousarray(inputs["ffn2_w_gate"][sl]), "wu2": np.ascontiguousarray(inputs["ffn2_w_up"][sl]),
        "wd2": np.ascontiguousarray(inputs["ffn2_w_down"][sl]),
        "w_in": np.ascontiguousarray(inputs["w_in"][sl]), "w_out": np.ascontiguousarray(inputs["w_out"][sl]),
        "fw": np.ascontiguousarray(inputs["fourier_w"][sl]), "pw": np.ascontiguousarray(inputs["pool_w"][sl]),
        "gtab": _gtab(inputs, layers), "ccsc": ccsc, "tw": tw, "invtab": inv,
    }
    in_maps = [dict(shared, xT=xT_cores[c]) for c in range(NCORES)]
    res = run_bass_kernel_spmd(nc, in_maps, core_ids=list(range(NCORES)))
    return [np.asarray(r["outT"]) for r in res.results]


N_LAUNCH_LAYERS = 4


def kernel(**inputs):
    inputs = {k: np.asarray(v) for k, v in inputs.items()}
    x = inputs["x"].astype(np.float32, copy=False)
    B = x.shape[0]
    Ltot = inputs["w_in"].shape[0]
    consts = _const_tables()
    xT_cores = [np.ascontiguousarray(x[c * NSEQ:(c + 1) * NSEQ].transpose(0, 2, 1)).reshape(NSEQ, KD, 128, S)
                for c in range(NCORES)]
    step = N_LAUNCH_LAYERS
    for l0 in range(0, Ltot, step):
        xT_cores = _run(step, list(range(l0, l0 + step)), xT_cores, inputs, consts)
    out = np.empty((B, S, D), dtype=np.float32)
    for c in range(NCORES):
        out[c * NSEQ:(c + 1) * NSEQ] = xT_cores[c].reshape(NSEQ, D, S).transpose(0, 2, 1)
    return out
```
